# Optimizing a Trainium2 kernel written in Bass

```python
import jax, jax.numpy as jnp
from jax import lax
import numpy as np

D_MODEL = 1024
BATCH = 2
SEQ = 16384
DEPTH = 2

CHUNK = 64
HG_HEADS = 4
HG_DK = 128
HG_W = HG_HEADS * HG_DK
RET_HEADS = 4
RET_DK = 128
RET_DV = 256
RET_QK = RET_HEADS * RET_DK
RET_V = RET_HEADS * RET_DV
ML_HEADS = 4
ML_DK = 128
ML_DV = 256
ML_QK = ML_HEADS * ML_DK
ML_V = ML_HEADS * ML_DV
CONV_W = 4
D_FF = 4 * D_MODEL
EPS = 1e-6
GATE_CAP = 15.0
ROPE_BASE = 10000.0
N_EVEN = (DEPTH + 1) // 2
N_ODD = DEPTH // 2
EVEN_IN = 4 * HG_W + 2 * RET_QK + 2 * RET_V
EVEN_OUT = HG_W + RET_V
ODD_IN = 2 * ML_QK + 2 * ML_V + 2 * ML_HEADS

kernel_name = 'hybrid_hgrn2_retnet_mlstm'


def rmsnorm(x, g):
    xf = x.astype(jnp.float32)
    y = xf * lax.rsqrt(jnp.mean(xf * xf, axis=-1, keepdims=True) + EPS)
    return (y * g.astype(jnp.float32)).astype(x.dtype)


def head_rmsnorm(o, g, n_heads):
    b, s, w = o.shape
    return rmsnorm(o.reshape(b, s, n_heads, w // n_heads), g).reshape(b, s, w)


def to_chunks(x, n_heads):
    b, s, w = x.shape
    return x.reshape(b, s // CHUNK, CHUNK, n_heads, w // n_heads).transpose(1, 0, 3, 2, 4)


def gate_chunks(g):
    b, s, h = g.shape
    return g.reshape(b, s // CHUNK, CHUNK, h).transpose(1, 0, 3, 2)


def from_chunks(x):
    nc, b, h, c, d = x.shape
    return x.transpose(1, 0, 3, 2, 4).reshape(b, nc * c, h * d)


def rope(x, n_heads):
    b, s, w = x.shape
    dk = w // n_heads
    half = dk // 2
    xh = x.reshape(b, s, n_heads, dk)
    inv_freq = ROPE_BASE ** (-jnp.arange(half, dtype=jnp.float32) / half)
    ang = jnp.arange(s, dtype=jnp.float32)[:, None] * inv_freq[None, :]
    cos = jnp.cos(ang)[None, :, None, :]
    sin = jnp.sin(ang)[None, :, None, :]
    x1, x2 = xh[..., :half], xh[..., half:]
    return jnp.concatenate([x1 * cos - x2 * sin, x2 * cos + x1 * sin], axis=-1).reshape(b, s, w)


def hgrn2_scan(q, k, logf, v):
    nc, b, h, c, dk = q.shape
    dv = v.shape[-1]
    mask = jnp.tril(jnp.ones((c, c), dtype=bool))

    def step(state, inp):
        qc, kc, lfc, vc = inp
        bcum = jnp.cumsum(lfc, axis=2)
        diff = bcum[:, :, :, None, :] - bcum[:, :, None, :, :]
        decay = jnp.exp(jnp.where(mask[:, :, None], diff, -jnp.inf))
        scores = jnp.einsum('bhtd,bhsd,bhtsd->bhts', qc, kc, decay)
        out = jnp.einsum('bhts,bhsv->bhtv', scores, vc) \
            + jnp.einsum('bhtd,bhdv->bhtv', qc * jnp.exp(bcum), state)
        blast = bcum[:, :, -1, :]
        state = jnp.exp(blast)[..., None] * state \
            + jnp.einsum('bhsd,bhsv->bhdv', kc * jnp.exp(blast[:, :, None, :] - bcum), vc)
        return state, out

    init = jnp.zeros((b, h, dk, dv), jnp.float32)
    _, out = lax.scan(step, init, (q, k, logf, v))
    return out


def retention_scan(q, k, v):
    nc, b, h, c, dk = q.shape
    dv = v.shape[-1]
    mask = jnp.tril(jnp.ones((c, c), dtype=bool))
    lg = jnp.log1p(-jnp.exp2(-5.0 - jnp.arange(h, dtype=jnp.float32)))
    j = jnp.arange(c, dtype=jnp.float32)
    decay = jnp.where(mask, jnp.exp(lg[:, None, None] * (j[:, None] - j[None, :])), 0.0)
    q_decay = jnp.exp(lg[:, None] * (j + 1.0))[:, :, None]
    k_decay = jnp.exp(lg[:, None] * (c - 1.0 - j))[:, :, None]
    chunk_decay = jnp.exp(lg * c)[:, None, None]

    def step(state, inp):
        qc, kc, vc = inp
        scores = jnp.einsum('bhtd,bhsd->bhts', qc, kc) * decay
        out = jnp.einsum('bhts,bhsv->bhtv', scores, vc) \
            + jnp.einsum('bhtd,bhdv->bhtv', qc * q_decay, state)
        state = chunk_decay * state + jnp.einsum('bhsd,bhsv->bhdv', kc * k_decay, vc)
        return state, out

    init = jnp.zeros((b, h, dk, dv), jnp.float32)
    _, out = lax.scan(step, init, (q, k, v))
    return out


def mlstm_scan(q, k, v, lf, ig):
    nc, b, h, c, dk = q.shape
    dv = v.shape[-1]
    mask = jnp.tril(jnp.ones((c, c), dtype=bool))

    def step(carry, inp):
        cm, n, m = carry
        qc, kc, vc, lfc, igc = inp
        bcum = jnp.cumsum(lfc, axis=-1)
        logw = jnp.where(mask, bcum[..., :, None] - bcum[..., None, :] + igc[..., None, :], -jnp.inf)
        inter = bcum + m[..., None]
        m_row = jnp.maximum(inter, jnp.max(logw, axis=-1))
        w_intra = jnp.exp(logw - m_row[..., None])
        w_inter = jnp.exp(inter - m_row)
        scores = jnp.einsum('bhtd,bhsd->bhts', qc, kc) * w_intra
        num = jnp.einsum('bhts,bhsv->bhtv', scores, vc) \
            + w_inter[..., None] * jnp.einsum('bhtd,bhdv->bhtv', qc, cm)
        den = jnp.sum(scores, axis=-1) + w_inter * jnp.einsum('bhtd,bhd->bht', qc, n)
        out = num / jnp.maximum(jnp.abs(den), jnp.exp(-m_row))[..., None]
        blast = bcum[..., -1]
        logw_end = blast[..., None] - bcum + igc
        m_new = jnp.maximum(blast + m, jnp.max(logw_end, axis=-1))
        carry_scale = jnp.exp(blast + m - m_new)
        kw = kc * jnp.exp(logw_end - m_new[..., None])[..., None]
        cm = carry_scale[..., None, None] * cm + jnp.einsum('bhsd,bhsv->bhdv', kw, vc)
        n = carry_scale[..., None] * n + jnp.sum(kw, axis=2)
        return (cm, n, m_new), out

    init = (jnp.zeros((b, h, dk, dv), jnp.float32),
            jnp.zeros((b, h, dk), jnp.float32),
            jnp.zeros((b, h), jnp.float32))
    _, out = lax.scan(step, init, (q, k, v, lf, ig))
    return out


def causal_conv_silu(x, w, bias):
    s = x.shape[1]
    xp = jnp.pad(x, ((0, 0), (CONV_W - 1, 0), (0, 0)))
    y = sum(w[j].astype(jnp.float32) * xp[:, j:j + s, :] for j in range(CONV_W))
    return jax.nn.silu(y + bias.astype(jnp.float32))


def soft_cap(z):
    return GATE_CAP * jnp.tanh(z / GATE_CAP)


def even_mixer(h, w_in, lb, hg_norm_g, ret_norm_g, w_out):
    p = (h @ w_in).astype(jnp.float32)
    splits = [HG_W, 2 * HG_W, 3 * HG_W, 4 * HG_W, 4 * HG_W + RET_QK,
              4 * HG_W + 2 * RET_QK, 4 * HG_W + 2 * RET_QK + RET_V]
    hq, hf, hi, hgate, rq, rk, rv, rgate = jnp.split(p, splits, axis=-1)
    f = lb + (1.0 - lb) * jax.nn.sigmoid(hf)
    o_hg = from_chunks(hgrn2_scan(to_chunks(jax.nn.silu(hq), HG_HEADS), to_chunks(1.0 - f, HG_HEADS),
                                  to_chunks(jnp.log(f), HG_HEADS), to_chunks(hi, HG_HEADS)))
    o_hg = head_rmsnorm(o_hg, hg_norm_g, HG_HEADS) * jax.nn.silu(hgate)
    rq = rope(rq, RET_HEADS)
    rk = rope(rk, RET_HEADS) * RET_DK ** -0.5
    o_ret = from_chunks(retention_scan(to_chunks(rq, RET_HEADS), to_chunks(rk, RET_HEADS),
                                       to_chunks(rv, RET_HEADS)))
    o_ret = head_rmsnorm(o_ret, ret_norm_g, RET_HEADS) * jax.nn.silu(rgate)
    y = jnp.concatenate([o_hg, o_ret], axis=-1).astype(h.dtype)
    return y @ w_out


def odd_mixer(h, w_in, conv_w, conv_b, gate_b, ml_norm_g, w_out):
    p = (h @ w_in).astype(jnp.float32)
    splits = [2 * ML_QK, 2 * ML_QK + ML_V, 2 * ML_QK + 2 * ML_V, 2 * ML_QK + 2 * ML_V + ML_HEADS]
    qk, v, og, ig_pre, fg_pre = jnp.split(p, splits, axis=-1)
    qk = causal_conv_silu(qk, conv_w, conv_b)
    q, k = jnp.split(qk, 2, axis=-1)
    k = k * ML_DK ** -0.5
    gb = gate_b.astype(jnp.float32)
    ig = soft_cap(ig_pre + gb[0])
    lf = jax.nn.log_sigmoid(soft_cap(fg_pre + gb[1]))
    out = from_chunks(mlstm_scan(to_chunks(q, ML_HEADS), to_chunks(k, ML_HEADS), to_chunks(v, ML_HEADS),
                                 gate_chunks(lf), gate_chunks(ig)))
    y = (jax.nn.sigmoid(og) * head_rmsnorm(out, ml_norm_g, ML_HEADS)).astype(h.dtype)
    return y @ w_out


def sq_relu_mlp(h, w1, w2):
    return jnp.square(jax.nn.relu(h @ w1)) @ w2


def setup_inputs(seed: int = 0) -> dict:
    key = jax.random.key(seed)
    ks = jax.random.split(key, 16)
    f32 = jnp.float32

    def nrm(k, shape, fan_in):
        return jax.random.normal(k, shape, f32) * fan_in ** -0.5

    x = jax.random.normal(ks[0], (BATCH, SEQ, D_MODEL), f32)
    norm_g = 1.0 + 0.05 * jax.random.normal(ks[1], (DEPTH, 4, D_MODEL), f32)
    hg_lower_bound = 0.5 * jax.random.normal(ks[2], (DEPTH + 1, HG_W), f32)
    ev_w_in = nrm(ks[3], (N_EVEN, D_MODEL, EVEN_IN), D_MODEL)
    ev_hg_norm_g = 1.0 + 0.05 * jax.random.normal(ks[4], (N_EVEN, HG_DK), f32)
    ev_ret_norm_g = 1.0 + 0.05 * jax.random.normal(ks[5], (N_EVEN, RET_DV), f32)
    ev_w_out = nrm(ks[6], (N_EVEN, EVEN_OUT, D_MODEL), EVEN_OUT)
    od_w_in = nrm(ks[7], (N_ODD, D_MODEL, ODD_IN), D_MODEL)
    od_conv_w = nrm(ks[8], (N_ODD, CONV_W, 2 * ML_QK), CONV_W)
    od_conv_b = 0.02 * jax.random.normal(ks[9], (N_ODD, 2 * ML_QK), f32)
    kg1, kg2 = jax.random.split(ks[10])
    i_bias = 0.1 * jax.random.normal(kg1, (N_ODD, ML_HEADS), f32)
    f_bias = jnp.linspace(3.0, 6.0, ML_HEADS, dtype=f32)[None, :] + 0.1 * jax.random.normal(kg2, (N_ODD, ML_HEADS), f32)
    od_gate_b = jnp.stack([i_bias, f_bias], axis=1)
    od_ml_norm_g = 1.0 + 0.05 * jax.random.normal(ks[11], (N_ODD, ML_DV), f32)
    od_w_out = nrm(ks[12], (N_ODD, ML_V, D_MODEL), ML_V)
    mlp_w1 = nrm(ks[13], (DEPTH, D_MODEL, D_FF), D_MODEL)
    mlp_w2 = nrm(ks[14], (DEPTH, D_FF, D_MODEL), D_FF)
    return {'x': x, 'norm_g': norm_g, 'hg_lower_bound': hg_lower_bound,
            'ev_w_in': ev_w_in, 'ev_hg_norm_g': ev_hg_norm_g, 'ev_ret_norm_g': ev_ret_norm_g,
            'ev_w_out': ev_w_out, 'od_w_in': od_w_in, 'od_conv_w': od_conv_w, 'od_conv_b': od_conv_b,
            'od_gate_b': od_gate_b, 'od_ml_norm_g': od_ml_norm_g, 'od_w_out': od_w_out,
            'mlp_w1': mlp_w1, 'mlp_w2': mlp_w2}


def reference(x, norm_g, hg_lower_bound, ev_w_in, ev_hg_norm_g, ev_ret_norm_g, ev_w_out,
              od_w_in, od_conv_w, od_conv_b, od_gate_b, od_ml_norm_g, od_w_out, mlp_w1, mlp_w2):
    lower_bounds = jnp.cumsum(jax.nn.softmax(hg_lower_bound.astype(jnp.float32), axis=0), axis=0)
    for l in range(DEPTH):
        g = norm_g[l]
        h = rmsnorm(x, g[0])
        if l % 2 == 0:
            e = l // 2
            y = even_mixer(h, ev_w_in[e], lower_bounds[l], ev_hg_norm_g[e], ev_ret_norm_g[e], ev_w_out[e])
        else:
            o = l // 2
            y = odd_mixer(h, od_w_in[o], od_conv_w[o], od_conv_b[o], od_gate_b[o], od_ml_norm_g[o], od_w_out[o])
        x = x + rmsnorm(y, g[1])
        h = rmsnorm(x, g[2])
        x = x + rmsnorm(sq_relu_mlp(h, mlp_w1[l], mlp_w2[l]), g[3])
    return x
```

```python
import numpy as np
from contextlib import ExitStack
import concourse.bass as bass
import concourse.mybir as mybir

F32 = mybir.dt.float32
BF16 = mybir.dt.bfloat16
AF = mybir.ActivationFunctionType
ALU = mybir.AluOpType
AX = mybir.AxisListType


class Tl:
    __slots__ = ("ap", "keys")

    def __init__(self, ap, keys):
        self.ap = ap
        if not isinstance(keys, (list, tuple)) or (len(keys) > 0 and not isinstance(keys[0], (tuple, list))):
            keys = [keys]
        self.keys = [tuple(k) if isinstance(k, list) else k for k in keys]


def _ap(x):
    return x.ap if isinstance(x, Tl) else x


def _keys(*xs):
    out = []
    for x in xs:
        if isinstance(x, Tl):
            out.extend(x.keys)
    return out


class Prog:
    def __init__(self, nc):
        self.nc = nc
        self.stack = ExitStack()
        self.ops = []
        self.last_writer = {}
        self.readers = {}
        self.eng_ops = {e: [] for e in ("pe", "act", "dve", "pool", "sp")}
        self.dma_groups = {}
        self.final_groups = []
        self._n = 0
        self.sb_off = 16512
        self.sb_limit = 229376
        self.bar_deps = set()

    def sbuf(self, name, shape, dtype):
        isz = 4 if dtype == F32 else 2
        n = 1
        for s in shape[1:]:
            n *= s
        nbytes = (n * isz + 63) // 64 * 64
        off = self.sb_off
        self.sb_off += nbytes
        assert self.sb_off <= self.sb_limit, (name, self.sb_off)
        self._n += 1
        return self.nc.alloc_sbuf_tensor_at("%s_%d" % (name, self._n), list(shape), dtype, offset=off)

    def mark(self):
        return self.sb_off

    def release(self, m):
        self.sb_off = m

    def barrier(self):
        deps = set()
        for e, lst in self.eng_ops.items():
            if lst:
                deps.add(lst[-1])
        for g, info in self.dma_groups.items():
            if info.get("last") is not None:
                deps.add(info["last"])
        self.bar_deps = deps
        self.last_writer = {}
        self.readers = {}

    def psum(self, name, shape, dtype=F32):
        return self.stack.enter_context(self.nc.psum_tensor(name, list(shape), dtype))

    def add(self, eng, fn, reads=(), writes=(), dma_group=None, final=False):
        idx = len(self.ops)
        deps = set()
        for k in reads:
            w = self.last_writer.get(k)
            if w is not None:
                deps.add((w, "raw"))
        for k in writes:
            w = self.last_writer.get(k)
            if w is not None:
                deps.add((w, "waw"))
            for r in self.readers.get(k, ()):
                if r != idx:
                    deps.add((r, "war"))
        for b in self.bar_deps:
            deps.add((b, "raw"))
        op = dict(idx=idx, eng=eng, fn=fn, deps=deps, dma=dma_group, signal=False,
                  pos=len(self.eng_ops[eng]))
        self.ops.append(op)
        self.eng_ops[eng].append(idx)
        for k in reads:
            self.readers.setdefault(k, []).append(idx)
        for k in writes:
            self.last_writer[k] = idx
            self.readers[k] = []
        if dma_group is not None:
            g = self.dma_groups.setdefault(dma_group, dict(n=0, sem=None, final=False))
            g["n"] += 1
            op["dma_count"] = g["n"] * 16
            g["last"] = idx
            if final:
                g["final"] = True
        return idx

    def dma(self, out, in_, group, eng="sp", final=False):
        o, i = _ap(out), _ap(in_)
        return self.add(eng, lambda e: e.dma_start(out=o, in_=i), reads=_keys(in_), writes=_keys(out),
                        dma_group=group, final=final)

    def matmul(self, out, lhsT, rhs, start=True, stop=True, extra_reads=()):
        o, l, r = _ap(out), _ap(lhsT), _ap(rhs)
        return self.add("pe", lambda e: e.matmul(o, l, r, start=start, stop=stop),
                        reads=_keys(lhsT, rhs) + ([] if start else _keys(out)) + list(extra_reads),
                        writes=_keys(out))

    def transpose(self, out, in_, ident):
        o, i, d = _ap(out), _ap(in_), _ap(ident)
        return self.add("pe", lambda e: e.transpose(o, i, d), reads=_keys(in_, ident), writes=_keys(out))

    def act(self, out, in_, func, bias=0.0, scale=1.0, accum_out=None, eng="act"):
        o, i, b, s, a = _ap(out), _ap(in_), _ap(bias), _ap(scale), _ap(accum_out)
        kw = {}
        if accum_out is not None:
            kw["accum_out"] = a
        return self.add(eng, lambda e: e.activation(o, i, func, bias=b, scale=s, **kw),
                        reads=_keys(in_, bias, scale), writes=_keys(out, accum_out))

    def tt(self, out, in0, in1, op, eng="dve"):
        o, a, b = _ap(out), _ap(in0), _ap(in1)
        return self.add(eng, lambda e: e.tensor_tensor(o, a, b, op), reads=_keys(in0, in1), writes=_keys(out))

    def ts(self, out, in0, s1, op0, s2=None, op1=None, eng="dve", accum_out=None):
        o, a, x1, x2 = _ap(out), _ap(in0), _ap(s1), _ap(s2)
        kw = {}
        if op1 is not None:
            kw["op1"] = op1
        if accum_out is not None:
            kw["accum_out"] = _ap(accum_out)
        return self.add(eng, lambda e: e.tensor_scalar(o, a, x1, x2, op0, **kw),
                        reads=_keys(in0, s1, s2), writes=_keys(out, accum_out))

    def stt(self, out, in0, scalar, in1, op0, op1, eng="dve"):
        o, a, s, b = _ap(out), _ap(in0), _ap(scalar), _ap(in1)
        return self.add(eng, lambda e: e.scalar_tensor_tensor(o, a, s, b, op0, op1),
                        reads=_keys(in0, scalar, in1), writes=_keys(out))

    def scan(self, out, d0, d1, init, op0, op1, eng="dve"):
        o, a, b, i0 = _ap(out), _ap(d0), _ap(d1), _ap(init)
        return self.add(eng, lambda e: e.tensor_tensor_scan(o, a, b, i0, op0, op1),
                        reads=_keys(d0, d1, init), writes=_keys(out))

    def copy(self, out, in_, eng="dve"):
        o, i = _ap(out), _ap(in_)
        if eng == "act":
            return self.add(eng, lambda e: e.copy(o, i), reads=_keys(in_), writes=_keys(out))
        return self.add(eng, lambda e: e.tensor_copy(o, i), reads=_keys(in_), writes=_keys(out))

    def recip(self, out, in_, eng="dve"):
        o, i = _ap(out), _ap(in_)
        return self.add(eng, lambda e: e.reciprocal(o, i), reads=_keys(in_), writes=_keys(out))

    def memset(self, out, val, eng="pool"):
        o = _ap(out)
        return self.add(eng, lambda e: e.memset(o, val), reads=(), writes=_keys(out))

    def emit(self):
        nc = self.nc
        ops = self.ops
        for op in ops:
            keep = set()
            for (d, kind) in op["deps"]:
                dop = ops[d]
                if dop["dma"] is None and dop["eng"] == op["eng"] and op["dma"] is None:
                    if op["eng"] == "pe":
                        continue
                    if kind != "raw":
                        continue
                    if op["eng"] != "pool" and op["pos"] - dop["pos"] > 2:
                        continue
                if dop["dma"] is None and dop["eng"] == op["eng"] and op["dma"] is not None:
                    pass
                keep.add(d)
            op["wdeps"] = keep
            for d in keep:
                ops[d]["signal"] = True
        cnt = {e: 0 for e in self.eng_ops}
        for op in ops:
            if op["dma"] is not None:
                gname = str(op["dma"])
                if gname.endswith("w") or gname == "const":
                    op["token"] = ("dma:" + gname, self.dma_groups[op["dma"]]["n"] * 16)
                else:
                    op["token"] = ("dma:" + gname, op["dma_count"])
            elif op["signal"]:
                cnt[op["eng"]] += 1
                op["token"] = ("eng:" + op["eng"], cnt[op["eng"]])
        sems = {}
        for e in self.eng_ops:
            sems["eng:" + e] = self.stack.enter_context(nc.semaphore("s_" + e))
        for g in self.dma_groups:
            sems["dma:" + str(g)] = self.stack.enter_context(nc.semaphore("d_" + str(g).replace(" ", "")))
        self.sems = sems
        prog = self

        def run(engname, e):
            waited = {}
            for idx in prog.eng_ops[engname]:
                op = ops[idx]
                need = {}
                for d in op["wdeps"]:
                    s, v = ops[d]["token"]
                    if need.get(s, 0) < v:
                        need[s] = v
                for s, v in need.items():
                    if waited.get(s, 0) < v:
                        e.wait_ge(sems[s], v)
                        waited[s] = v
                ins = op["fn"](e)
                if op["dma"] is not None:
                    ins.then_inc(sems["dma:" + str(op["dma"])], 16)
                elif op["signal"]:
                    ins.then_inc(sems["eng:" + engname], 1)
            if engname == "sp":
                for g, info in prog.dma_groups.items():
                    if info["final"]:
                        e.wait_ge(sems["dma:" + str(g)], info["n"] * 16)

        with nc.Block() as block:
            @block.tensor
            def _(e):
                run("pe", e)

            @block.scalar
            def _(e):
                run("act", e)

            @block.vector
            def _(e):
                run("dve", e)

            @block.gpsimd
            def _(e):
                run("pool", e)

            @block.sync
            def _(e):
                run("sp", e)
        self.stack.close()


import numpy as np
import math

D = 1024
EPS = 1e-6


class Ctx:
    pass


def setup_common(P, dr):
    C = Ctx()
    C.P = P
    nc = P.nc
    C.ps = [P.psum("ps%d" % i, [128, 512], F32) for i in range(8)]
    C.ident = P.sbuf("ident", [128, 128], BF16)
    P.dma(Tl(C.ident[:], ("ident",)), dr["c_ident"], "const", eng="pool")
    C.T_ident = Tl(C.ident[:], ("ident",))
    epsT = P.sbuf("epsT", [128, 1], F32)
    P.memset(Tl(epsT[:], ("eps",)), EPS)
    P.eps_ap = Tl(epsT[:], ("eps",))
    return C


def rms_rstd(P, ss, rstd, n, width):
    P.act(rstd, ss, AF.Ln, bias=P.eps_ap, scale=1.0 / width)
    P.act(rstd, rstd, AF.Exp, scale=-0.5)


def phase_mlp(P, C, dr, l, src, dst, ntok, TT=256, final=False, tag="mlp"):
    m0 = P.mark()
    NS = TT // 128
    w1b = P.sbuf("w1b", [128, 8, 4096], BF16)
    w2b = P.sbuf("w2b", [128, 32, 1024], BF16)
    w1v = dr["mlp_w1"][l].rearrange("(k p) f -> p k f", p=128)
    w2v = dr["mlp_w2"][l].rearrange("(k p) f -> p k f", p=128)
    for k in range(8):
        P.dma(Tl(w1b[:, k, :], ("w1b", k)), w1v[:, k, :], tag + "w", eng="pool")
    for k in range(0, 32, 4):
        P.dma(Tl(w2b[:, k:k + 4, :], [("w2b", kk) for kk in range(k, k + 4)]), w2v[:, k:k + 4, :], tag + "w", eng="pool")
    g2b = P.sbuf("g2b", [128, D], F32)
    g3b = P.sbuf("g3b", [128, D], F32)
    P.dma(Tl(g2b[:], ("g2b",)), dr["norm_g"][l, 2:3, :].to_broadcast([128, D]), tag + "w", eng="pool")
    P.dma(Tl(g3b[:], ("g3b",)), dr["norm_g"][l, 3:4, :].to_broadcast([128, D]), tag + "w", eng="pool")
    X = [P.sbuf("X%d" % s, [128, NS, D], F32) for s in range(2)]
    hb = P.sbuf("hb", [128, NS, D], BF16)
    hT = P.sbuf("hT", [128, 8, TT], BF16)
    aT = P.sbuf("aT", [128, 32, TT], BF16)
    junk = P.sbuf("junk", [128, D], BF16)
    sq = [P.sbuf("sq%d" % s, [128, TT], F32) for s in range(2)]
    y = [P.sbuf("y%d" % s, [128, D], F32) for s in range(2)]
    tmp = P.sbuf("tmp", [128, D], F32)
    st = P.sbuf("st", [128, 4 * NS], F32)
    ps = C.ps
    srcv = src.rearrange("(m j p) d -> m p j d", p=128, j=NS)
    dstv = dst.rearrange("(m j p) d -> m p j d", p=128, j=NS)
    NT = ntok // TT
    pi = 0
    for m in range(NT):
        s = m % 2
        Xk = lambda j: ("X", s, j)
        P.dma(Tl(X[s][:], [Xk(j) for j in range(NS)]), srcv[m], tag + "x%d" % s)
        ssk = ("st", "ss")
        for j in range(NS):
            P.act(Tl(junk[:], ("junk",)), Tl(X[s][:, j, :], Xk(j)), AF.Square,
                  accum_out=Tl(st[:, j:j + 1], ("st", "ss", j)))
        P.act(Tl(st[:, NS:2 * NS], ("st", "r")), Tl(st[:, 0:NS], [("st", "ss", j) for j in range(NS)]), AF.Ln,
              bias=P.eps_ap, scale=1.0 / D)
        P.act(Tl(st[:, NS:2 * NS], ("st", "r")), Tl(st[:, NS:2 * NS], ("st", "r")), AF.Exp, scale=-0.5)
        for j in range(NS):
            P.stt(Tl(hb[:, j, :], ("hb", j)), Tl(X[s][:, j, :], Xk(j)), Tl(st[:, NS + j:NS + j + 1], ("st", "r")),
                  Tl(g2b[:], ("g2b",)), ALU.mult, ALU.mult)
            pb = ps[pi % 8]; pk = ("ps", pi % 8); pi += 1
            pbv = pb[:].bitcast(BF16).rearrange("p (k t) -> p k t", t=128)
            for k in range(8):
                P.transpose(Tl(pbv[:, k, :], pk), Tl(hb[:, j, k * 128:(k + 1) * 128], ("hb", j)), C.T_ident)
            P.copy(Tl(hT[:, :, j * 128:(j + 1) * 128], ("hT", j)), Tl(pbv, pk), eng="act" if j % 2 else "dve")
        hTk = [("hT", j) for j in range(NS)]
        for fb in range(32):
            pb = ps[pi % 8]; pk = ("ps", pi % 8); pi += 1
            for k in range(8):
                P.matmul(Tl(pb[:, 0:TT], pk), Tl(w1b[:, k, fb * 128:(fb + 1) * 128], ("w1b", k)),
                         Tl(hT[:, k, :], hTk), start=(k == 0), stop=(k == 7))
            q = sq[fb % 2]; qk = ("sq", fb % 2)
            P.act(Tl(q[:], qk), Tl(pb[:, 0:TT], pk), AF.Square)
            P.stt(Tl(aT[:, fb, :], ("aT", fb)), Tl(pb[:, 0:TT], pk), 0.0, Tl(q[:], qk), ALU.is_gt, ALU.mult)
        for j in range(NS):
            yy = y[j % 2]; yk = ("y", j % 2)
            for cg in range(2):
                pb = ps[pi % 8]; pk = ("ps", pi % 8); pi += 1
                for fb in range(32):
                    P.matmul(Tl(pb[:], pk), Tl(aT[:, fb, j * 128:(j + 1) * 128], ("aT", fb)),
                             Tl(w2b[:, fb, cg * 512:(cg + 1) * 512], ("w2b", fb)), start=(fb == 0), stop=(fb == 31))
                P.copy(Tl(yy[:, cg * 512:(cg + 1) * 512], yk), Tl(pb[:], pk), eng="act")
            P.act(Tl(junk[:], ("junk",)), Tl(yy[:], yk), AF.Square, accum_out=Tl(st[:, 2 * NS + j:2 * NS + j + 1], ("st", "ss2", j)))
            P.act(Tl(st[:, 3 * NS + j:3 * NS + j + 1], ("st", "r2", j)), Tl(st[:, 2 * NS + j:2 * NS + j + 1], ("st", "ss2", j)),
                  AF.Ln, bias=P.eps_ap, scale=1.0 / D)
            P.act(Tl(st[:, 3 * NS + j:3 * NS + j + 1], ("st", "r2", j)), Tl(st[:, 3 * NS + j:3 * NS + j + 1], ("st", "r2", j)),
                  AF.Exp, scale=-0.5)
            P.tt(Tl(tmp[:], ("tmp",)), Tl(yy[:], yk), Tl(g3b[:], ("g3b",)), ALU.mult, eng="pool")
            P.stt(Tl(X[s][:, j, :], Xk(j)), Tl(tmp[:], ("tmp",)), Tl(st[:, 3 * NS + j:3 * NS + j + 1], ("st", "r2", j)),
                  Tl(X[s][:, j, :], Xk(j)), ALU.mult, ALU.add)
        P.dma(dstv[m], Tl(X[s][:], [Xk(j) for j in range(NS)]), tag + "o%d" % s, final=final)
    P.barrier()
    P.release(m0)


LNKS = -0.5 * math.log(128.0)


class Rot:
    def __init__(self, ps, banks):
        self.ps = ps; self.banks = banks; self.i = 0

    def next(self):
        b = self.banks[self.i % len(self.banks)]
        self.i += 1
        return self.ps[b], ("ps", b)


def load_tokens_norm_T(P, C, X, s, NS, TT, st, gb, hb, hT, rot, tagk):
    Xk = lambda j: ("X", s, j)
    for j in range(NS):
        P.act(Tl(C.junk[:], ("junk",)), Tl(X[s][:, j, :], Xk(j)), AF.Square,
              accum_out=Tl(st[:, j:j + 1], ("st", "ss", j)))
    P.act(Tl(st[:, NS:2 * NS], ("st", "r")), Tl(st[:, 0:NS], [("st", "ss", j) for j in range(NS)]), AF.Ln,
          bias=P.eps_ap, scale=1.0 / D)
    P.act(Tl(st[:, NS:2 * NS], ("st", "r")), Tl(st[:, NS:2 * NS], ("st", "r")), AF.Exp, scale=-0.5)
    for j in range(NS):
        P.stt(Tl(hb[:, j, :], ("hb", j)), Tl(X[s][:, j, :], Xk(j)), Tl(st[:, NS + j:NS + j + 1], ("st", "r")),
              gb, ALU.mult, ALU.mult)
        pb, pk = rot.next()
        pbv = pb[:].bitcast(BF16).rearrange("p (k t) -> p k t", t=128)
        for k in range(8):
            P.transpose(Tl(pbv[:, k, :], pk), Tl(hb[:, j, k * 128:(k + 1) * 128], ("hb", j)), C.T_ident)
        P.copy(Tl(hT[:, :, j * 128:(j + 1) * 128], ("hT", j)), Tl(pbv, pk), eng="act" if j % 2 else "dve")


def phase_mix(P, C, dr, layer, src, dst, ntok, TT=256, final=False, dbg=None):
    tag = "mx%d" % layer
    m0 = P.mark()
    NS = TT // 128
    NCH = TT // 64
    ps = C.ps
    rot = Rot(ps, [0, 1, 2])
    SC, UB, O0, O1, O2 = 3, 4, 5, 6, 7
    OB = [O0, O1, O2]
    oTl = [(lambda c0, c1, b=b: Tl(ps[OB[b]][:, c0:c1], ("ps", OB[b]))) for b in range(3)]
    if layer == 0:
        NCOL = 6144
        NOB = 12
        heads = [("hg", h) for h in range(4)] + [("ret", h) for h in range(4)]
    else:
        NCOL = 3080
        NOB = 8
        heads = [("ml", h) for h in range(4)]
    win = P.sbuf("win", [128, 8, NCOL], BF16)
    wout = P.sbuf("wout", [128, NOB, D], BF16)
    if layer == 0:
        wv = dr["ev_w_in"][0].rearrange("(k p) f -> p k f", p=128)
        for k in range(8):
            P.dma(Tl(win[:, k, 0:5120], ("win", k)), wv[:, k, :], tag + "w", eng="pool")
        for qi, base in enumerate((2048, 2560)):
            for h in range(4):
                c0 = 5120 + qi * 512 + h * 128
                s0 = base + h * 128
                P.dma(Tl(win[:, :, c0:c0 + 64], ("winp", qi, h, 0)), wv[:, :, s0 + 64:s0 + 128], tag + "w", eng="pool")
                P.dma(Tl(win[:, :, c0 + 64:c0 + 128], ("winp", qi, h, 1)), wv[:, :, s0:s0 + 64], tag + "w", eng="pool")
        wov = dr["ev_w_out"][0].rearrange("(k p) f -> p k f", p=128)
    else:
        wv = dr["od_w_in"][0].rearrange("(k p) f -> p k f", p=128)
        for k in range(8):
            P.dma(Tl(win[:, k, :], ("win", k)), wv[:, k, :], tag + "w", eng="pool")
        wov = dr["od_w_out"][0].rearrange("(k p) f -> p k f", p=128)
    for k in range(0, NOB, 4):
        P.dma(Tl(wout[:, k:k + 4, :], [("wout", kk) for kk in range(k, k + 4)]), wov[:, k:k + 4, :], tag + "w", eng="pool")
    WIN = lambda k: ("win", k)
    g0b = P.sbuf("g0b", [128, D], F32)
    g1b = P.sbuf("g1b", [128, D], F32)
    P.dma(Tl(g0b[:], ("g0b",)), dr["norm_g"][layer, 0:1, :].to_broadcast([128, D]), tag + "w", eng="pool")
    P.dma(Tl(g1b[:], ("g1b",)), dr["norm_g"][layer, 1:2, :].to_broadcast([128, D]), tag + "w", eng="pool")
    T_g0b = Tl(g0b[:], ("g0b",)); T_g1b = Tl(g1b[:], ("g1b",))
    mask = P.sbuf("mask", [128, 128], F32)
    P.dma(Tl(mask[:], ("mask",)), dr["c_mask"], tag + "w")
    T_mask = Tl(mask[:], ("mask",))
    rmask = P.sbuf("rmask", [128, TT], F32)
    P.memset(Tl(rmask[:], ("rmask",)), 1.0)
    P.memset(Tl(rmask[:].rearrange("p (c t) -> p c t", t=64)[:, :, 0:1], ("rmask",)), 0.0)
    T_rmask = Tl(rmask[:], ("rmask",))
    onesb = P.sbuf("onesb", [128, 128], BF16)
    P.memset(Tl(onesb[:], ("onesb",)), 1.0)
    T_ones = Tl(onesb[:], ("onesb",))
    one1 = P.sbuf("one1", [128, 1], F32)
    P.memset(Tl(one1[:], ("one1",)), 1.0)
    T_one1 = Tl(one1[:], ("one1",))
    X = [P.sbuf("X%d" % s, [128, NS, D], F32) for s in range(2)]
    hb = P.sbuf("hb", [128, NS, D], BF16)
    hT = P.sbuf("hT", [128, 8, TT], BF16)
    st = P.sbuf("st", [128, 4 * NS], F32)
    Yt = P.sbuf("Yt", [128, D], F32)
    yT = P.sbuf("yT", [128, NOB, TT], BF16)
    fA = [P.sbuf("fA%d" % i, [128, TT], F32) for i in range(6)]
    TfA = [Tl(fA[i][:], ("fA", i)) for i in range(6)]
    E1 = P.sbuf("E1", [128, TT], F32); E2 = P.sbuf("E2", [128, TT], F32); E3 = P.sbuf("E3", [128, TT], F32)
    T_E1 = Tl(E1[:], ("E1",)); T_E2 = Tl(E2[:], ("E2",)); T_E3 = Tl(E3[:], ("E3",))
    qT = P.sbuf("qT", [128, TT], BF16); kT = P.sbuf("kT", [128, TT], BF16); khT = P.sbuf("khT", [128, TT], BF16)
    T_qT = Tl(qT[:], ("qT",)); T_kT = Tl(kT[:], ("kT",)); T_khT = Tl(khT[:], ("khT",))
    khat = P.sbuf("khat", [128, NS, 128], BF16)
    PT = P.sbuf("PT", [128, 128], BF16)
    sqb = P.sbuf("sqb", [128, 2, TT], BF16)
    dec = P.sbuf("dec", [128, NCH], F32); cexp = P.sbuf("cexp", [128, NCH], F32)
    hTk = [("hT", j) for j in range(NS)]

    def projF(c0, M=128):
        pb, pk = rot.next()
        for k in range(8):
            wk = WIN(k) if c0 < 5120 or layer != 0 else [("winp", (c0 - 5120) // 512, ((c0 - 5120) % 512) // 128, 0), ("winp", (c0 - 5120) // 512, ((c0 - 5120) % 512) // 128, 1)]
            P.matmul(Tl(pb[0:M, 0:TT], pk), Tl(win[:, k, c0:c0 + M], wk), Tl(hT[:, k, :], hTk),
                     start=(k == 0), stop=(k == 7))
        return Tl(pb[0:M, 0:TT], pk)

    def projT(j, c0, n):
        pb, pk = rot.next()
        for k in range(8):
            P.matmul(Tl(pb[:, 0:n], pk), Tl(hT[:, k, j * 128:(j + 1) * 128], ("hT", j)), Tl(win[:, k, c0:c0 + n], WIN(k)),
                     start=(k == 0), stop=(k == 7))
        return Tl(pb[:, 0:n], pk)

    def sigmoid_from(pt, out, neg_bias=None):
        if neg_bias is None:
            P.act(out, pt, AF.Exp, scale=-1.0)
        else:
            P.act(out, pt, AF.Exp, scale=-1.0, bias=neg_bias)
        P.ts(out, out, 1.0, ALU.add)
        P.recip(out, out)

    c3 = lambda t: t.rearrange("p (c t) -> p c t", t=64)

    if layer == 0:
        lbz = P.sbuf("lbz", [128, 3, 4], F32)
        P.dma(Tl(lbz[:], ("lbz",)), dr["p_lbz"], tag + "w")
        lbt = P.sbuf("lbt", [128, 4, 4], F32)
        P.act(Tl(lbz[:], ("lbz",)), Tl(lbz[:], ("lbz",)), AF.Exp)
        P.tt(Tl(lbt[:, 3, :], ("lbt", 3)), Tl(lbz[:, 0, :], ("lbz",)), Tl(lbz[:, 1, :], ("lbz",)), ALU.add)
        P.tt(Tl(lbt[:, 3, :], ("lbt", 3)), Tl(lbt[:, 3, :], ("lbt", 3)), Tl(lbz[:, 2, :], ("lbz",)), ALU.add)
        P.recip(Tl(lbt[:, 3, :], ("lbt", 3)), Tl(lbt[:, 3, :], ("lbt", 3)))
        P.tt(Tl(lbt[:, 0, :], ("lbt", 0)), Tl(lbz[:, 0, :], ("lbz",)), Tl(lbt[:, 3, :], ("lbt", 3)), ALU.mult)
        P.ts(Tl(lbt[:, 1, :], ("lbt", 1)), Tl(lbt[:, 0, :], ("lbt", 0)), -1.0, ALU.mult, 1.0, ALU.add)
        P.ts(Tl(lbt[:, 2, :], ("lbt", 2)), Tl(lbt[:, 0, :], ("lbt", 0)), 1.0, ALU.mult, -1.0, ALU.add)
        hgn = P.sbuf("hgn", [128, 1], F32)
        P.dma(Tl(hgn[:], ("hgn",)), dr["p_hgn"], tag + "w")
        rgn = P.sbuf("rgn", [128, 2], F32)
        P.dma(Tl(rgn[:], ("rgn",)), dr["p_rgn"], tag + "w")
        rtab = P.sbuf("rtab", [128, 4, 3, 64], F32)
        P.dma(Tl(rtab[:], ("rtab",)), dr["c_rtab"], tag + "w")
        rdc = P.sbuf("rdc", [128, 4, 2], F32)
        P.dma(Tl(rdc[:], ("rdc",)), dr["c_rdc"], tag + "w")
        cosT = [P.sbuf("cosT%d" % s, [128, TT], F32) for s in range(2)]
        sinT = [P.sbuf("sinT%d" % s, [128, TT], F32) for s in range(2)]
        vh = P.sbuf("vh", [128, NS, 512], BF16)
        vr = P.sbuf("vr", [128, NS, 1024], BF16)
        S_hg = P.sbuf("S_hg", [128, 4, 128], F32)
        S_rt = P.sbuf("S_rt", [128, 4, 256], F32)
        P.memset(Tl(S_hg[:], [("S_hg", h) for h in range(4)]), 0.0)
        P.memset(Tl(S_rt[:], [("S_rt", h) for h in range(4)]), 0.0)
        Sb = P.sbuf("Sb", [128, 256], BF16)
    else:
        mgn = P.sbuf("mgn", [128, 2], F32)
        P.dma(Tl(mgn[:], ("mgn",)), dr["p_mgn"], tag + "w")
        convw = P.sbuf("convw", [128, 8, 4], F32)
        P.dma(Tl(convw[:], ("convw",)), dr["p_convw"], tag + "w")
        convb = P.sbuf("convb", [128, 2, 8], F32)
        P.dma(Tl(convb[:, 0, :], ("convb",)), dr["p_convb"], tag + "w")
        P.ts(Tl(convb[:, 1, :], ("convb",)), Tl(convb[:, 0, :], ("convb",)), -1.0, ALU.mult)
        gbs = P.sbuf("gbs", [4, 4], F32)
        P.dma(Tl(gbs[:, 0:2], ("gbs",)), dr["p_gateb"], tag + "w")
        P.ts(Tl(gbs[:, 0:2], ("gbs",)), Tl(gbs[:, 0:2], ("gbs",)), 2.0 / 15.0, ALU.mult)
        sel = P.sbuf("sel", [4, 4, 128], F32)
        P.dma(Tl(sel[:], ("sel",)), dr["c_sel"], tag + "w")
        lnks = P.sbuf("lnks", [128, 1], F32)
        P.memset(Tl(lnks[:], ("lnks",)), LNKS)
        pbuf = P.sbuf("pbuf", [128, 8, TT + 3], F32)
        P.memset(Tl(pbuf[:], [("pbuf", c) for c in range(8)]), 0.0)
        cq = P.sbuf("cq", [128, 8, TT], F32)
        G = [P.sbuf("G%d" % i, [4, TT], F32) for i in range(6)]
        TG = [Tl(G[i][:], ("G", i)) for i in range(6)]
        vml = P.sbuf("vml", [128, NS, 4, 384], BF16)
        P.memset(Tl(vml[:], [("vml", j) for j in range(NS)]), 1.0)
        S_ml = P.sbuf("S_ml", [128, 4, 384], F32)
        P.memset(Tl(S_ml[:], [("S_ml", h) for h in range(4)]), 0.0)
        Sb = P.sbuf("Sb", [128, 384], BF16)
        rmask4 = Tl(rmask[0:4, :], ("rmask",))
    C.junk = P.sbuf("junk", [128, D], BF16)

    srcv = src.rearrange("(m j p) d -> m p j d", p=128, j=NS)
    dstv = dst.rearrange("(m j p) d -> m p j d", p=128, j=NS)
    NT = ntok // TT

    def gla(hname, nb, S_t, vfn, vrows, dec_ap, cexp_ap, oT):
        for j in range(NS):
            sc = Tl(ps[SC][:, (j % 4) * 128:(j % 4 + 1) * 128], ("ps", SC, j % 4))
            P.matmul(sc, Tl(kT[:, j * 128:(j + 1) * 128], ("kT",)), Tl(qT[:, j * 128:(j + 1) * 128], ("qT",)))
            P.tt(Tl(PT[:], ("PT",)), sc, T_mask, ALU.mult)
            for blk in range(nb):
                P.matmul(oT[blk](j * 128, (j + 1) * 128), vfn(j, blk), Tl(PT[:], ("PT",)), start=True, stop=False)
            for cc in range(2):
                ci = 2 * j + cc
                P.act(Tl(Sb[:, 0:nb * 128], ("Sb",)), S_t, AF.Identity, scale=cexp_ap(ci))
                for blk in range(nb):
                    P.matmul(oT[blk](ci * 64, (ci + 1) * 64), Tl(Sb[:, blk * 128:(blk + 1) * 128], ("Sb",)),
                             Tl(qT[:, ci * 64:(ci + 1) * 64], ("qT",)), start=False, stop=(cc == 1))
                U = Tl(ps[UB][:, 0:nb * 128], ("ps", UB))
                P.matmul(U, Tl(khat[cc * 64:(cc + 1) * 64, j, :], ("khat",)), vrows(j, cc))
                P.stt(S_t, S_t, dec_ap(ci), U, ALU.mult, ALU.add)

    def khat_transposes():
        pb, pk = rot.next()
        pbv = pb[:].bitcast(BF16).rearrange("p (k t) -> p k t", t=128)
        for j in range(NS):
            P.transpose(Tl(pbv[:, j, :], pk), Tl(khT[:, j * 128:(j + 1) * 128], ("khT",)), C.T_ident)
        P.copy(Tl(khat[:], ("khat",)), Tl(pbv[:, 0:NS, :], pk), eng="act")

    def head_norm_gate(o_list, gain_aps, gate_fn, blk0):
        nb = len(o_list)
        for b in range(nb):
            P.tt(Tl(sqb[:, b, :], ("sqb", b)), o_list[b], o_list[b], ALU.mult, eng="pool")
        ssum = Tl(ps[UB][:, 0:TT], ("ps", UB))
        for b in range(nb):
            P.matmul(ssum, T_ones, Tl(sqb[:, b, :], ("sqb", b)), start=(b == 0), stop=(b == nb - 1))
        rs = TfA[4]
        P.act(rs, ssum, AF.Ln, bias=P.eps_ap, scale=1.0 / (128.0 * nb))
        P.act(rs, rs, AF.Exp, scale=-0.5)
        for b in range(nb):
            gt = gate_fn(b)
            P.stt(o_list[b], o_list[b], gain_aps[b], rs, ALU.mult, ALU.mult)
            P.tt(Tl(yT[:, blk0 + b, :], ("yT", blk0 + b)), o_list[b], gt, ALU.mult, eng="pool")

    for m in range(NT):
        s = m % 2
        Xk = lambda j: ("X", s, j)
        P.dma(Tl(X[s][:], [Xk(j) for j in range(NS)]), srcv[m], tag + "x%d" % s)
        if layer == 0:
            P.dma(Tl(cosT[s][:], ("cosT", s)), dr["c_cos"][:, m * TT:(m + 1) * TT], tag + "rc%d" % s)
            P.dma(Tl(sinT[s][:], ("sinT", s)), dr["c_sin"][:, m * TT:(m + 1) * TT], tag + "rs%d" % s)
        load_tokens_norm_T(P, C, X, s, NS, TT, st, T_g0b, hb, hT, rot, tag)
        if layer == 0:
            for j in range(NS):
                pt = projT(j, 1024, 512)
                P.copy(Tl(vh[:, j, :], ("vh", j)), pt, eng="act")
                for half in range(2):
                    pt = projT(j, 3072 + half * 512, 512)
                    P.copy(Tl(vr[:, j, half * 512:(half + 1) * 512], ("vr", j)), pt, eng="act" if half else "dve")
            for (kind, h) in heads:
                if kind == "hg":
                    pf = projF(512 + h * 128)
                    r = TfA[0]
                    sigmoid_from(pf, r)
                    kf = TfA[1]
                    P.ts(kf, r, Tl(lbt[:, 2, h:h + 1], ("lbt", 2)), ALU.mult, Tl(lbt[:, 1, h:h + 1], ("lbt", 1)), ALU.add)
                    g = TfA[2]
                    P.act(g, r, AF.Ln, bias=Tl(lbt[:, 0, h:h + 1], ("lbt", 0)), scale=Tl(lbt[:, 1, h:h + 1], ("lbt", 1)))
                    b = TfA[3]
                    P.scan(b, T_rmask, g, 0.0, ALU.mult, ALU.add)
                    b3 = c3(fA[3][:])
                    d1 = TfA[0]
                    P.tt(Tl(c3(fA[0][:]), ("fA", 0)), Tl(b3, ("fA", 3)), Tl(b3[:, :, 31:32].to_broadcast([128, NCH, 64]), ("fA", 3)), ALU.subtract)
                    P.act(T_E1, d1, AF.Exp)
                    P.act(T_E2, d1, AF.Exp, scale=-1.0)
                    P.tt(Tl(c3(fA[0][:]), ("fA", 0)), Tl(b3[:, :, 63:64].to_broadcast([128, NCH, 64]), ("fA", 3)), Tl(b3, ("fA", 3)), ALU.subtract)
                    P.act(T_E3, d1, AF.Exp)
                    P.act(Tl(dec[:], ("dec",)), Tl(b3[:, :, 63], ("fA", 3)), AF.Exp)
                    P.act(Tl(cexp[:], ("cexp",)), Tl(b3[:, :, 31], ("fA", 3)), AF.Exp)
                    pq = projF(h * 128)
                    sg = TfA[0]
                    sigmoid_from(pq, sg)
                    P.tt(sg, sg, T_E1, ALU.mult, eng="pool")
                    P.tt(T_qT, pq, sg, ALU.mult)
                    P.tt(T_kT, kf, T_E2, ALU.mult, eng="pool")
                    P.tt(T_khT, kf, T_E3, ALU.mult, eng="pool")
                    khat_transposes()
                    gla("hg", 1, Tl(S_hg[:, h, :], ("S_hg", h)),
                        lambda j, blk, h=h: Tl(vh[:, j, h * 128:(h + 1) * 128], ("vh", j)),
                        lambda j, cc, h=h: Tl(vh[cc * 64:(cc + 1) * 64, j, h * 128:(h + 1) * 128], ("vh", j)),
                        lambda ci: Tl(dec[:, ci:ci + 1], ("dec",)), lambda ci: Tl(cexp[:, ci:ci + 1], ("cexp",)), oTl)
                    o0 = TfA[2]
                    P.copy(o0, oTl[0](0, TT), eng="act")

                    def gate_fn(b, h=h):
                        pg = projF(1536 + h * 128)
                        gt = TfA[5]
                        sigmoid_from(pg, gt)
                        P.tt(gt, pg, gt, ALU.mult)
                        return gt
                    head_norm_gate([o0], [Tl(hgn[:, 0:1], ("hgn",))], gate_fn, h)
                else:
                    def roped(c0, cp0, out):
                        pa = projF(c0)
                        P.tt(TfA[0], pa, Tl(cosT[s][:], ("cosT", s)), ALU.mult)
                        pp = projF(cp0)
                        P.tt(TfA[1], pp, Tl(sinT[s][:], ("sinT", s)), ALU.mult)
                        P.tt(out, TfA[0], TfA[1], ALU.add, eng="pool")
                    roped(2048 + h * 128, 5120 + h * 128, TfA[2])
                    tb = lambda i: Tl(rtab[:, h, i:i + 1, :].to_broadcast([128, NCH, 64]), ("rtab",))
                    P.tt(Tl(c3(qT[:]), ("qT",)), Tl(c3(fA[2][:]), ("fA", 2)), tb(0), ALU.mult)
                    roped(2560 + h * 128, 5632 + h * 128, TfA[3])
                    P.tt(Tl(c3(kT[:]), ("kT",)), Tl(c3(fA[3][:]), ("fA", 3)), tb(1), ALU.mult, eng="pool")
                    P.tt(Tl(c3(khT[:]), ("khT",)), Tl(c3(fA[3][:]), ("fA", 3)), tb(2), ALU.mult, eng="pool")
                    khat_transposes()
                    gla("ret", 2, Tl(S_rt[:, h, :], ("S_rt", h)),
                        lambda j, blk, h=h: Tl(vr[:, j, h * 256 + blk * 128:h * 256 + (blk + 1) * 128], ("vr", j)),
                        lambda j, cc, h=h: Tl(vr[cc * 64:(cc + 1) * 64, j, h * 256:(h + 1) * 256], ("vr", j)),
                        lambda ci: Tl(rdc[:, h, 0:1], ("rdc",)), lambda ci: Tl(rdc[:, h, 1:2], ("rdc",)), oTl)
                    o0 = TfA[2]; o1 = TfA[3]
                    P.copy(o0, oTl[0](0, TT), eng="act")
                    P.copy(o1, oTl[1](0, TT), eng="act")

                    def gate_fn(b, h=h):
                        pg = projF(4096 + h * 256 + b * 128)
                        gt = TfA[5]
                        sigmoid_from(pg, gt)
                        P.tt(gt, pg, gt, ALU.mult)
                        return gt
                    head_norm_gate([o0, o1], [Tl(rgn[:, 0:1], ("rgn",)), Tl(rgn[:, 1:2], ("rgn",))], gate_fn, 4 + 2 * h)
        else:
            for j in range(NS):
                for half in range(2):
                    pt = projT(j, 1024 + half * 512, 512)
                    P.copy(Tl(vml[:, j, 2 * half:2 * half + 2, 0:256], ("vml", j)),
                           Tl(pt.ap.rearrange("p (h v) -> p h v", v=256), pt.keys), eng="act" if half else "dve")
            for cb in range(8):
                pc = projF(cb * 128)
                pbk = ("pbuf", cb)
                P.copy(Tl(pbuf[:, cb, 3:TT + 3], pbk), pc, eng="act")
                acc = TfA[0]
                P.ts(acc, Tl(pbuf[:, cb, 0:TT], pbk), Tl(convw[:, cb, 0:1], ("convw",)), ALU.mult)
                for jj in range(1, 4):
                    P.stt(acc, Tl(pbuf[:, cb, jj:jj + TT], pbk), Tl(convw[:, cb, jj:jj + 1], ("convw",)), acc, ALU.mult, ALU.add)
                sg = TfA[1]
                sigmoid_from(acc, sg, neg_bias=Tl(convb[:, 1, cb:cb + 1], ("convb",)))
                P.stt(Tl(cq[:, cb, :], ("cq", cb)), acc, Tl(convb[:, 0, cb:cb + 1], ("convb",)), sg, ALU.add, ALU.mult)
                P.copy(Tl(pbuf[:, cb, 0:3], pbk), Tl(pbuf[:, cb, TT:TT + 3], pbk), eng="pool")
            pig = projF(3072, M=4)
            pfg = projF(3076, M=4)

            def softcap(pt, bias_ap, out):
                P.act(out, pt, AF.Exp, scale=2.0 / 15.0, bias=bias_ap)
                P.ts(out, out, 1.0, ALU.add)
                P.recip(out, out)
                P.ts(out, out, -30.0, ALU.mult, 15.0, ALU.add)
            ig = TG[0]
            softcap(pig, Tl(gbs[:, 0:1], ("gbs",)), ig)
            nlf = TG[1]
            softcap(pfg, Tl(gbs[:, 1:2], ("gbs",)), nlf)
            P.act(nlf, nlf, AF.Exp, scale=-1.0)
            P.act(nlf, nlf, AF.Ln, bias=Tl(one1[0:4, :], ("one1",)))
            bn = TG[2]
            P.scan(bn, rmask4, nlf, 0.0, ALU.mult, ALU.add)
            bn3 = c3(G[2][:])
            d1n = TG[3]
            P.tt(Tl(c3(G[3][:]), ("G", 3)), Tl(bn3, ("G", 2)), Tl(bn3[:, :, 31:32].to_broadcast([4, NCH, 64]), ("G", 2)), ALU.subtract)
            t2 = TG[4]
            P.tt(t2, d1n, ig, ALU.add)
            t3 = TG[5]
            P.tt(Tl(c3(G[5][:]), ("G", 5)), Tl(bn3, ("G", 2)), Tl(bn3[:, :, 63:64].to_broadcast([4, NCH, 64]), ("G", 2)), ALU.subtract)
            P.tt(t3, t3, ig, ALU.add)
            for (kind, h) in heads:
                selh = Tl(sel[:, h, :], ("sel",))
                pb, pk = rot.next()
                P.matmul(Tl(pb[:, 0:TT], pk), selh, d1n)
                P.act(T_E1, Tl(pb[:, 0:TT], pk), AF.Exp, scale=-1.0)
                pb, pk = rot.next()
                P.matmul(Tl(pb[:, 0:TT], pk), selh, t2)
                P.act(T_E2, Tl(pb[:, 0:TT], pk), AF.Exp, bias=Tl(lnks[:], ("lnks",)))
                pb, pk = rot.next()
                P.matmul(Tl(pb[:, 0:TT], pk), selh, t3)
                P.act(T_E3, Tl(pb[:, 0:TT], pk), AF.Exp, bias=Tl(lnks[:], ("lnks",)))
                pb, pk = rot.next()
                P.matmul(Tl(pb[:, 0:NCH], pk), selh, Tl(bn3[:, :, 63], ("G", 2)))
                P.matmul(Tl(pb[:, NCH:2 * NCH], pk), selh, Tl(bn3[:, :, 31], ("G", 2)))
                P.act(Tl(dec[:], ("dec",)), Tl(pb[:, 0:NCH], pk), AF.Exp, scale=-1.0)
                P.act(Tl(cexp[:], ("cexp",)), Tl(pb[:, NCH:2 * NCH], pk), AF.Exp, scale=-1.0)
                P.tt(T_qT, Tl(cq[:, h, :], ("cq", h)), T_E1, ALU.mult)
                P.tt(T_kT, Tl(cq[:, 4 + h, :], ("cq", 4 + h)), T_E2, ALU.mult, eng="pool")
                P.tt(T_khT, Tl(cq[:, 4 + h, :], ("cq", 4 + h)), T_E3, ALU.mult, eng="pool")
                khat_transposes()
                gla("ml", 3, Tl(S_ml[:, h, :], ("S_ml", h)),
                    lambda j, blk, h=h: Tl(vml[:, j, h, blk * 128:(blk + 1) * 128], ("vml", j)),
                    lambda j, cc, h=h: Tl(vml[cc * 64:(cc + 1) * 64, j, h, :], ("vml", j)),
                    lambda ci: Tl(dec[:, ci:ci + 1], ("dec",)), lambda ci: Tl(cexp[:, ci:ci + 1], ("cexp",)), oTl)
                dn = TfA[1]
                P.act(dn, oTl[2](0, TT), AF.Abs)
                P.ts(dn, dn, 1.0, ALU.max)
                P.recip(dn, dn)
                o0 = TfA[2]; o1 = TfA[3]
                P.tt(o0, oTl[0](0, TT), dn, ALU.mult)
                P.tt(o1, oTl[1](0, TT), dn, ALU.mult)

                def gate_fn(b, h=h):
                    pg = projF(2048 + h * 256 + b * 128)
                    gt = TfA[5]
                    sigmoid_from(pg, gt)
                    return gt
                head_norm_gate([o0, o1], [Tl(mgn[:, 0:1], ("mgn",)), Tl(mgn[:, 1:2], ("mgn",))], gate_fn, 2 * h)
        if dbg is not None and m == dbg.get("m", 0):
            P.dma(dbg["yT"], Tl(yT[:], [("yT", b) for b in range(NOB)]), "dbg", eng="pool", final=True)
        for j in range(NS):
            for cg in range(2):
                pb, pk = rot.next()
                for ob in range(NOB):
                    P.matmul(Tl(pb[:], pk), Tl(yT[:, ob, j * 128:(j + 1) * 128], ("yT", ob)),
                             Tl(wout[:, ob, cg * 512:(cg + 1) * 512], ("wout", ob)), start=(ob == 0), stop=(ob == NOB - 1))
                P.copy(Tl(Yt[:, cg * 512:(cg + 1) * 512], ("Yt",)), Tl(pb[:], pk), eng="act")
            ssj = Tl(st[:, 2 * NS + j:2 * NS + j + 1], ("st", "ss2", j))
            rj = Tl(st[:, 3 * NS + j:3 * NS + j + 1], ("st", "r2", j))
            P.act(Tl(C.junk[:], ("junk",)), Tl(Yt[:], ("Yt",)), AF.Square, accum_out=ssj)
            P.act(rj, ssj, AF.Ln, bias=P.eps_ap, scale=1.0 / D)
            P.act(rj, rj, AF.Exp, scale=-0.5)
            P.tt(Tl(Yt[:], ("Yt",)), Tl(Yt[:], ("Yt",)), T_g1b, ALU.mult, eng="pool")
            P.stt(Tl(X[s][:, j, :], Xk(j)), Tl(Yt[:], ("Yt",)), rj, Tl(X[s][:, j, :], Xk(j)), ALU.mult, ALU.add)
        P.dma(dstv[m], Tl(X[s][:], [Xk(j) for j in range(NS)]), tag + "o%d" % s, final=final)
    P.barrier()
    P.release(m0)


def host_consts(pos0, ntok):
    c = {}
    c["c_ident"] = np.eye(128, dtype=np.float32)
    s = np.arange(128)[:, None]; t = np.arange(128)[None, :]
    c["c_mask"] = ((s // 64 == t // 64) & (s <= t)).astype(np.float32)
    half = 64
    inv_freq = (np.float32(10000.0) ** (-(np.arange(half, dtype=np.float32)) / np.float32(half))).astype(np.float32)
    pos = np.arange(pos0, pos0 + ntok, dtype=np.float32)
    ang = (pos[None, :] * inv_freq[:, None]).astype(np.float32)
    cos = np.cos(ang).astype(np.float32); sin = np.sin(ang).astype(np.float32)
    c["c_cos"] = np.concatenate([cos, cos], 0)
    c["c_sin"] = np.concatenate([-sin, sin], 0)
    lg = np.log1p(-np.exp2(-5.0 - np.arange(4, dtype=np.float64)))
    tl = np.arange(64, dtype=np.float64)
    ks = 128.0 ** -0.5
    rtab = np.zeros((128, 4, 3, 64), np.float32)
    rdc = np.zeros((128, 4, 2), np.float32)
    for h in range(4):
        rtab[:, h, 0, :] = np.exp(lg[h] * (tl - 31.0))[None, :]
        rtab[:, h, 1, :] = (np.exp(lg[h] * (31.0 - tl)) * ks)[None, :]
        rtab[:, h, 2, :] = (np.exp(lg[h] * (63.0 - tl)) * ks)[None, :]
        rdc[:, h, 0] = np.exp(lg[h] * 64.0)
        rdc[:, h, 1] = np.exp(lg[h] * 32.0)
    c["c_rtab"] = rtab; c["c_rdc"] = rdc
    sel = np.zeros((4, 4, 128), np.float32)
    for h in range(4):
        sel[h, h, :] = 1.0
    c["c_sel"] = sel
    return c

def host_layout_params(inp):
    o = {}
    o["p_lbz"] = np.ascontiguousarray(inp["hg_lower_bound"].reshape(3, 4, 128).transpose(2, 0, 1))
    o["p_hgn"] = np.ascontiguousarray(inp["ev_hg_norm_g"].reshape(1, 128).T)
    o["p_rgn"] = np.ascontiguousarray(inp["ev_ret_norm_g"].reshape(2, 128).T)
    o["p_mgn"] = np.ascontiguousarray(inp["od_ml_norm_g"].reshape(2, 128).T)
    o["p_convw"] = np.ascontiguousarray(inp["od_conv_w"].reshape(4, 8, 128).transpose(2, 1, 0))
    o["p_convb"] = np.ascontiguousarray(inp["od_conv_b"].reshape(8, 128).T)
    o["p_gateb"] = np.ascontiguousarray(inp["od_gate_b"].reshape(2, 4).T)
    return o


from concourse.bass_utils import run_bass_kernel_spmd

SEQ_T = 16384
IN_SHAPES = {"x": [SEQ_T, 1024], "norm_g": [2, 4, 1024], "ev_w_in": [1, 1024, 5120], "ev_w_out": [1, 1536, 1024],
             "od_w_in": [1, 1024, 3080], "od_w_out": [1, 1024, 1024], "mlp_w1": [2, 1024, 4096], "mlp_w2": [2, 4096, 1024],
             "c_ident": [128, 128], "c_mask": [128, 128], "c_cos": [128, SEQ_T], "c_sin": [128, SEQ_T],
             "c_rtab": [128, 4, 3, 64], "c_rdc": [128, 4, 2], "c_sel": [4, 4, 128], "p_lbz": [128, 3, 4], "p_hgn": [128, 1],
             "p_rgn": [128, 2], "p_mgn": [128, 2], "p_convw": [128, 8, 4], "p_convb": [128, 8], "p_gateb": [4, 2]}


def build_program(ntok=SEQ_T):
    nc = bass.Bass("TRN2", target_bir_lowering=False)
    dr = {}
    for k, v in IN_SHAPES.items():
        shp = list(v)
        if k == "x":
            shp = [ntok, 1024]
        if k in ("c_cos", "c_sin"):
            shp = [128, ntok]
        dr[k] = nc.dram_tensor(k, shp, F32, kind="ExternalInput").ap()
    dr["out"] = nc.dram_tensor("out", [ntok, 1024], F32, kind="ExternalOutput").ap()
    xa = nc.dram_tensor("scr_a", [ntok, 1024], F32, kind="ExternalOutput").ap()
    xb = nc.dram_tensor("scr_b", [ntok, 1024], F32, kind="ExternalOutput").ap()
    P = Prog(nc)
    C = setup_common(P, dr)
    P.barrier()
    import os
    ph = os.environ.get("K_PHASES", "full")
    if ph == "full":
        phase_mix(P, C, dr, 0, dr["x"], xa, ntok, TT=256)
        phase_mlp(P, C, dr, 0, xa, xb, ntok, TT=256, tag="mlp0")
        phase_mix(P, C, dr, 1, xb, xa, ntok, TT=256)
        phase_mlp(P, C, dr, 1, xa, dr["out"], ntok, TT=256, final=True, tag="mlp1")
    elif ph == "B":
        phase_mix(P, C, dr, 0, dr["x"], xa, ntok, TT=256)
        phase_mlp(P, C, dr, 0, xa, dr["out"], ntok, TT=256, final=True, tag="mlp0")
    elif ph == "D":
        phase_mix(P, C, dr, 0, dr["x"], dr["out"], ntok, TT=256, final=True)
    elif ph == "C":
        phase_mlp(P, C, dr, 0, dr["x"], xa, ntok, TT=256, tag="mlp0")
        phase_mlp(P, C, dr, 1, xa, dr["out"], ntok, TT=256, final=True, tag="mlp1")
    P.emit()
    return nc


def kernel(**inputs):
    inp = {k: np.asarray(v) for k, v in inputs.items()}
    B = inp["x"].shape[0]
    ntok = inp["x"].shape[1]
    nc = build_program(ntok)
    consts = host_consts(0, ntok)
    params = host_layout_params(inp)
    in_maps = []
    for b in range(B):
        m = {"x": inp["x"][b]}
        for k in ("norm_g", "ev_w_in", "ev_w_out", "od_w_in", "od_w_out", "mlp_w1", "mlp_w2"):
            m[k] = inp[k]
        m.update(consts)
        m.update(params)
        in_maps.append({k: np.ascontiguousarray(v, dtype=np.float32) for k, v in m.items()})
    res = run_bass_kernel_spmd(nc, in_maps, core_ids=list(range(B)))
    out = np.stack([np.asarray(res.results[b]["out"], dtype=np.float32) for b in range(B)], axis=0)
    return out
```
